# Optimizing a Trainium2 kernel written in Bass

```python
import math
import jax
import jax.numpy as jnp
from jax import lax
import numpy as np

D_MODEL = 2048
BATCH = 2
SEQ = 8192
DEPTH = 1

GRID_W = 64
CTX_LEN = 256
HEAD_DIM = 128
ATTN_WIDTH = D_MODEL // 2
HYENA_WIDTH = D_MODEL - ATTN_WIDTH
MIX_WIDTH = ATTN_WIDTH + HYENA_WIDTH
N_Q_HEADS = ATTN_WIDTH // HEAD_DIM
N_KV_HEADS = 2
KV_GROUP = N_Q_HEADS // N_KV_HEADS
KV_WIDTH = N_KV_HEADS * HEAD_DIM
Q_END = ATTN_WIDTH
K_END = Q_END + KV_WIDTH
V_END = K_END + KV_WIDTH
IN_WIDTH = V_END + 3 * HYENA_WIDTH
Q_BLOCK = 128
ROPE_THETA = 10000.0
ROPE_AXIS_DIM = HEAD_DIM // 2
SHORT_CONV = 3
FILTER_EMB = 33
FILTER_BANDS = (FILTER_EMB - 1) // 2
FILTER_HIDDEN = 64
DECAY_TARGET = 1e-2
FAST_DECAY_PCT = 0.3
SLOW_DECAY_PCT = 1.5
MIN_DECAY = -math.log(DECAY_TARGET) / SLOW_DECAY_PCT
MAX_DECAY = -math.log(DECAY_TARGET) / FAST_DECAY_PCT
D_FF = 5632
N_MOD = 9
EPS = 1e-6
F32 = jnp.float32

kernel_name = "hymba_hyena_gqa_macaron_dit_layer"


def rms_norm(x, g):
    xf = x.astype(F32)
    y = xf * lax.rsqrt(jnp.mean(xf * xf, axis=-1, keepdims=True) + EPS)
    return (y * g.astype(F32)).astype(x.dtype)


def chunk(m, i):
    return m[..., i * D_MODEL:(i + 1) * D_MODEL]


def modulate(h, shift, scale):
    return h * (1 + scale) + shift


def swiglu(h, w_up, w_down):
    gate, up = jnp.split(h @ w_up, 2, axis=-1)
    return (jax.nn.silu(gate) * up) @ w_down


def half_ffn(s, m, slot, g, w_up, w_down):
    h = modulate(rms_norm(s, g), chunk(m, 3 * slot), chunk(m, 3 * slot + 1))
    return s + 0.5 * chunk(m, 3 * slot + 2) * swiglu(h, w_up, w_down)


def axial_rope_angles(L):
    rows = L // GRID_W
    row = jnp.repeat(jnp.arange(rows, dtype=jnp.int32), GRID_W)
    col = jnp.tile(jnp.arange(GRID_W, dtype=jnp.int32), rows)
    inv = ROPE_THETA ** (-jnp.arange(0, ROPE_AXIS_DIM, 2, dtype=F32) / ROPE_AXIS_DIM)
    ang = jnp.concatenate([row.astype(F32)[:, None] * inv, col.astype(F32)[:, None] * inv], axis=-1)
    return jnp.cos(ang), jnp.sin(ang)


def apply_rope(x, cos, sin):
    xf = x.astype(F32).reshape(*x.shape[:-1], HEAD_DIM // 2, 2)
    x0, x1 = xf[..., 0], xf[..., 1]
    c = cos[None, :, None, :]
    s = sin[None, :, None, :]
    out = jnp.stack([x0 * c - x1 * s, x0 * s + x1 * c], axis=-1)
    return out.reshape(x.shape).astype(x.dtype)


def q_heads(p_q, q_norm):
    B, L = p_q.shape[:2]
    return rms_norm(p_q.reshape(B, L, N_Q_HEADS, HEAD_DIM), q_norm)


def kv_heads(p_kv, k_norm):
    B, L = p_kv.shape[:2]
    k = rms_norm(p_kv[..., :KV_WIDTH].reshape(B, L, N_KV_HEADS, HEAD_DIM), k_norm)
    v = p_kv[..., KV_WIDTH:].reshape(B, L, N_KV_HEADS, HEAD_DIM)
    return k, v


def split_proj(p, q_norm, k_norm):
    q = q_heads(p[..., :Q_END], q_norm)
    k, v = kv_heads(p[..., Q_END:V_END], k_norm)
    return q, k, v, p[..., V_END:]


def block_attention(q, k, v):
    B, Lq = q.shape[:2]
    nblk = Lq // Q_BLOCK
    qb = q.reshape(B, nblk, Q_BLOCK, N_KV_HEADS, KV_GROUP, HEAD_DIM).transpose(1, 0, 2, 3, 4, 5)
    kf = k.astype(F32)
    vf = v.astype(F32)
    scale = HEAD_DIM ** -0.5

    def one_block(qblk):
        s = jnp.einsum('bqkgd,bskd->bkgqs', qblk.astype(F32), kf) * scale
        p = jax.nn.softmax(s, axis=-1)
        return jnp.einsum('bkgqs,bskd->bqkgd', p, vf).astype(q.dtype)

    ob = lax.map(one_block, qb)
    return ob.transpose(1, 0, 2, 3, 4, 5).reshape(B, Lq, N_Q_HEADS * HEAD_DIM)


def hyena_filters(L, w1, b1, w2, b2, w3, b3, w4, freq, decay):
    t = jnp.linspace(0.0, 1.0, L, dtype=F32)[:, None]
    w = 2.0 * math.pi * jnp.arange(L, dtype=F32)[:, None] / L
    f = jnp.linspace(1e-4, FILTER_BANDS - 1, FILTER_BANDS, dtype=F32)[None, :]
    z = jnp.concatenate([t, jnp.cos(f * w), -jnp.sin(f * w)], axis=-1)
    fr = freq.astype(F32)
    h = jnp.sin(fr * (z @ w1.astype(F32) + b1.astype(F32)))
    h = jnp.sin(fr * (h @ w2.astype(F32) + b2.astype(F32)))
    h = jnp.sin(fr * (h @ w3.astype(F32) + b3.astype(F32)))
    h = (h @ w4.astype(F32)).reshape(L, 2, HYENA_WIDTH)
    h = h * jnp.exp(-t[:, :, None] * jnp.abs(decay.astype(F32))[None])
    k_fwd, k_bwd = h[:, 0], h[:, 1]
    kk = jnp.concatenate([k_fwd, jnp.zeros((1, HYENA_WIDTH), F32), k_bwd[:0:-1]], axis=0)
    return kk / jnp.sum(jnp.abs(kk), axis=0, keepdims=True)


def long_conv(v, kk):
    L = v.shape[1]
    vf = jnp.fft.rfft(v.astype(F32), n=2 * L, axis=1)
    kf = jnp.fft.rfft(kk, n=2 * L, axis=0)
    return jnp.fft.irfft(vf * kf[None], n=2 * L, axis=1)[:, :L].astype(v.dtype)


def short_conv(u, w, b):
    L = u.shape[1]
    up = jnp.pad(u, ((0, 0), (1, 1), (0, 0)))
    return up[:, :L] * w[0] + up[:, 1:L + 1] * w[1] + up[:, 2:] * w[2] + b


def hyena_mixer(u, kk, conv_w, conv_b, hy_bias):
    uc = short_conv(u, conv_w, conv_b)
    x0, x1, v = jnp.split(uc, 3, axis=-1)
    v = v * x1
    v = long_conv(v, kk) + hy_bias * v
    return v * x0


def merge_groups(attn, hyo, g_out, w_out):
    y = jnp.concatenate([rms_norm(attn, g_out[:ATTN_WIDTH]), rms_norm(hyo, g_out[ATTN_WIDTH:])], axis=-1)
    return y @ w_out


def setup_inputs(seed: int = 0) -> dict:
    key = jax.random.key(seed)
    ks = jax.random.split(key, 32)

    def nrm(k, shape, fan_in, mult=1.0):
        return jax.random.normal(k, shape, F32) * (mult * fan_in ** -0.5)

    def gain(k, shape):
        return 1.0 + 0.05 * jax.random.normal(k, shape, F32)

    def small(k, shape):
        return 0.02 * jax.random.normal(k, shape, F32)

    return {
        "x": jax.random.normal(ks[0], (BATCH, SEQ, D_MODEL), F32),
        "c": jax.random.normal(ks[1], (BATCH, D_MODEL), F32),
        "ctx": jax.random.normal(ks[2], (BATCH, CTX_LEN, D_MODEL), F32),
        "c_ctx": jax.random.normal(ks[3], (D_MODEL,), F32),
        "w_ada": nrm(ks[4], (DEPTH, D_MODEL, N_MOD * D_MODEL), D_MODEL, 0.5),
        "b_ada": small(ks[5], (DEPTH, N_MOD * D_MODEL)),
        "g_norm": gain(ks[6], (DEPTH, 3, D_MODEL)),
        "w_ffn1_up": nrm(ks[7], (DEPTH, D_MODEL, 2 * D_FF), D_MODEL),
        "w_ffn1_down": nrm(ks[8], (DEPTH, D_FF, D_MODEL), D_FF),
        "w_ffn2_up": nrm(ks[9], (DEPTH, D_MODEL, 2 * D_FF), D_MODEL),
        "w_ffn2_down": nrm(ks[10], (DEPTH, D_FF, D_MODEL), D_FF),
        "w_in": nrm(ks[11], (DEPTH, D_MODEL, IN_WIDTH), D_MODEL),
        "q_norm": gain(ks[12], (DEPTH, HEAD_DIM)),
        "k_norm": gain(ks[13], (DEPTH, HEAD_DIM)),
        "conv_w": nrm(ks[14], (DEPTH, SHORT_CONV, 3 * HYENA_WIDTH), SHORT_CONV),
        "conv_b": small(ks[15], (DEPTH, 3 * HYENA_WIDTH)),
        "flt_w1": nrm(ks[16], (DEPTH, FILTER_EMB, FILTER_HIDDEN), FILTER_EMB, 2.0),
        "flt_b1": small(ks[17], (DEPTH, FILTER_HIDDEN)),
        "flt_w2": nrm(ks[18], (DEPTH, FILTER_HIDDEN, FILTER_HIDDEN), FILTER_HIDDEN, 2.0),
        "flt_b2": small(ks[19], (DEPTH, FILTER_HIDDEN)),
        "flt_w3": nrm(ks[20], (DEPTH, FILTER_HIDDEN, FILTER_HIDDEN), FILTER_HIDDEN, 2.0),
        "flt_b3": small(ks[21], (DEPTH, FILTER_HIDDEN)),
        "flt_w4": nrm(ks[22], (DEPTH, FILTER_HIDDEN, 2 * HYENA_WIDTH), FILTER_HIDDEN),
        "flt_freq": gain(ks[23], (DEPTH, FILTER_HIDDEN)),
        "flt_decay": jax.random.uniform(ks[24], (DEPTH, 2, HYENA_WIDTH), F32, MIN_DECAY, MAX_DECAY),
        "hy_bias": jax.random.normal(ks[25], (DEPTH, HYENA_WIDTH), F32),
        "g_out": gain(ks[26], (DEPTH, MIX_WIDTH)),
        "w_out": nrm(ks[27], (DEPTH, MIX_WIDTH, D_MODEL), MIX_WIDTH),
    }


def reference(x, c, ctx, c_ctx, w_ada, b_ada, g_norm, w_ffn1_up, w_ffn1_down, w_ffn2_up, w_ffn2_down,
              w_in, q_norm, k_norm, conv_w, conv_b, flt_w1, flt_b1, flt_w2, flt_b2, flt_w3, flt_b3,
              flt_w4, flt_freq, flt_decay, hy_bias, g_out, w_out):
    L = x.shape[1]
    Lc = ctx.shape[1]
    cos, sin = axial_rope_angles(L)
    s_lat = jax.nn.silu(c)
    s_ctx = jax.nn.silu(c_ctx)[None]
    for l in range(DEPTH):
        update_ctx = l < DEPTH - 1
        mod = (s_lat @ w_ada[l] + b_ada[l])[:, None, :]
        mod_c = (s_ctx @ w_ada[l] + b_ada[l])[:, None, :]
        flt = (flt_w1[l], flt_b1[l], flt_w2[l], flt_b2[l], flt_w3[l], flt_b3[l], flt_w4[l], flt_freq[l], flt_decay[l])

        x = half_ffn(x, mod, 0, g_norm[l, 0], w_ffn1_up[l], w_ffn1_down[l])
        ctx = half_ffn(ctx, mod_c, 0, g_norm[l, 0], w_ffn1_up[l], w_ffn1_down[l])

        h = modulate(rms_norm(x, g_norm[l, 1]), chunk(mod, 3), chunk(mod, 4))
        hc = modulate(rms_norm(ctx, g_norm[l, 1]), chunk(mod_c, 3), chunk(mod_c, 4))
        q, k, v, hy = split_proj(h @ w_in[l], q_norm[l], k_norm[l])
        if update_ctx:
            qc, kc, vc, hyc = split_proj(hc @ w_in[l], q_norm[l], k_norm[l])
        else:
            kc, vc = kv_heads(hc @ w_in[l, :, Q_END:V_END], k_norm[l])
        q = apply_rope(q, cos, sin)
        k = apply_rope(k, cos, sin)
        k_all = jnp.concatenate([kc, k], axis=1)
        v_all = jnp.concatenate([vc, v], axis=1)
        attn = block_attention(q, k_all, v_all)
        hyo = hyena_mixer(hy, hyena_filters(L, *flt), conv_w[l], conv_b[l], hy_bias[l])
        x = x + chunk(mod, 5) * merge_groups(attn, hyo, g_out[l], w_out[l])
        if update_ctx:
            attn_c = block_attention(qc, kc, vc)
            hyo_c = hyena_mixer(hyc, hyena_filters(Lc, *flt), conv_w[l], conv_b[l], hy_bias[l])
            ctx = ctx + chunk(mod_c, 5) * merge_groups(attn_c, hyo_c, g_out[l], w_out[l])
            ctx = half_ffn(ctx, mod_c, 2, g_norm[l, 2], w_ffn2_up[l], w_ffn2_down[l])

        x = half_ffn(x, mod, 2, g_norm[l, 2], w_ffn2_up[l], w_ffn2_down[l])
    return x
```

```python
import contextlib
import math
import numpy as np
import concourse.bass as bass
import concourse.mybir as mybir
from concourse.bass_utils import run_bass_kernel_spmd

F32 = mybir.dt.float32
BF16 = mybir.dt.bfloat16
I32 = mybir.dt.int32
AF = mybir.ActivationFunctionType
ALU = mybir.AluOpType
AX = mybir.AxisListType

D = 2048
DC = 16
DFF = 5632
FC = 44
SEQ = 8192
CTX = 256
NCORE = 8
TOK = 2048
HD = 128
INW = 4608
EPS = 1e-6
ENG_NAMES = ("pe", "act", "dve", "pool", "sp")


class Dep:
    __slots__ = ("w", "r")

    def __init__(self):
        self.w = None
        self.r = {}


class MS:
    N_DMA_SEMS = 10

    def __init__(self, nc):
        self.nc = nc
        self.ops = {e: [] for e in ENG_NAMES}
        self.count = {e: 0 for e in ENG_NAMES}
        self.known = {e: {} for e in ENG_NAMES}
        self.dma_cnt = {}
        self.dma_rr = {e: 0 for e in ENG_NAMES}
        self.semkeys = set()

    def _need(self, eng, tok, waits, skip_same=False):
        if tok is None:
            return
        key, val = tok
        if skip_same and key == ("c", eng):
            return
        if self.known[eng].get(key, 0) >= val:
            return
        self.known[eng][key] = val
        waits.append((key, val))

    def _collect(self, eng, reads, writes):
        waits = []
        for d in reads:
            self._need(eng, d.w, waits)
        for d in writes:
            self._need(eng, d.w, waits, skip_same=True)
            for key, val in d.r.items():
                self._need(eng, (key, val), waits, skip_same=True)
        m = {}
        for k, v in waits:
            m[k] = max(m.get(k, 0), v)
        return list(m.items())

    def _commit(self, tok, reads, writes):
        key, val = tok
        for d in reads:
            d.r[key] = max(d.r.get(key, 0), val)
        for d in writes:
            d.w = tok
            d.r = {}

    def op(self, eng, fn, reads=(), writes=()):
        waits = self._collect(eng, reads, writes)
        self.count[eng] += 1
        tok = (("c", eng), self.count[eng])
        self.semkeys.add(tok[0])
        self.ops[eng].append((waits, fn, (tok[0], 1)))
        self._commit(tok, reads, writes)

    def dma(self, eng, fn, reads=(), writes=()):
        j = self.dma_rr[eng]
        self.dma_rr[eng] = (j + 1) % self.N_DMA_SEMS
        key = ("d", eng, j)
        self.semkeys.add(key)
        prev = self.dma_cnt.get(key, 0)
        waits = self._collect(eng, reads, writes)
        if prev > 0 and self.known[eng].get(key, 0) < prev:
            self.known[eng][key] = prev
            waits.append((key, prev))
        tok = (key, prev + 16)
        self.dma_cnt[key] = prev + 16
        self.ops[eng].append((waits, fn, (key, 16)))
        self._commit(tok, reads, writes)

    def wait_all(self, eng, deps):
        waits = self._collect(eng, deps, ())
        self.ops[eng].append((waits, None, None))

    def emit(self):
        nc = self.nc
        with contextlib.ExitStack() as st:
            sems = {}
            for key in sorted(self.semkeys, key=str):
                sems[key] = st.enter_context(nc.semaphore("s_" + "_".join(str(x) for x in key)))
            block = st.enter_context(nc.Block())

            def run(eng_name):
                def body(e):
                    for waits, fn, inc in self.ops[eng_name]:
                        for key, val in waits:
                            e.wait_ge(sems[key], val)
                        if fn is None:
                            continue
                        ins = fn(e)
                        ins.then_inc(sems[inc[0]], inc[1])
                return body

            block.tensor(run("pe"))
            block.scalar(run("act"))
            block.vector(run("dve"))
            block.gpsimd(run("pool"))
            block.sync(run("sp"))


class Ctx:
    def __init__(self):
        self.nc = bass.Bass("TRN2", target_bir_lowering=False)
        self.ms = MS(self.nc)
        self.st = contextlib.ExitStack()
        self.ins = {}
        self.outs = {}
        self.out_deps = []

    def inp(self, name, shape, dt=F32):
        t = self.nc.dram_tensor(name, list(shape), dt, kind="ExternalInput")
        self.ins[name] = t
        return t.ap()

    def out(self, name, shape, dt=F32):
        t = self.nc.dram_tensor(name, list(shape), dt, kind="ExternalOutput")
        self.outs[name] = t
        d = Dep()
        self.out_deps.append(d)
        return t.ap(), d

    def sb(self, name, shape, dt):
        return self.st.enter_context(self.nc.sbuf_tensor(name, list(shape), dt))

    def ps(self, name, shape=(128, 512), dt=F32):
        return self.st.enter_context(self.nc.psum_tensor(name, list(shape), dt))

    def finish(self):
        self.ms.wait_all("sp", self.out_deps)
        self.ms.emit()
        self.st.close()


def bcast_row(ap2d_tensor, offset, n, parts=128):
    return bass.AP(tensor=ap2d_tensor, offset=offset, ap=[[0, parts], [1, n]])


class Common:
    def __init__(self, cx):
        self.cx = cx
        nc, ms = cx.nc, cx.ms
        self.ident_in = cx.inp("ident", [128, 128])
        self.ident_f = cx.sb("ident_f", [128, 128], F32)
        self.ident_b = cx.sb("ident_b", [128, 128], BF16)
        self.d_ident = Dep()
        ms.dma("sp", lambda e: e.dma_start(out=self.ident_f[:], in_=self.ident_in[:, :]), writes=[self.d_ident])
        ms.op("dve", lambda e: e.tensor_copy(out=self.ident_b[:], in_=self.ident_f[:]), reads=[self.d_ident], writes=[self.d_ident])
        self.banks = [cx.ps(f"bank{i}") for i in range(8)]
        self.d_bank = [Dep() for _ in range(8)]
        self.eps_t = cx.sb("eps_t", [128, 1], F32)
        self.d_eps = Dep()
        ms.op("dve", lambda e: e.memset(self.eps_t[:], EPS), writes=[self.d_eps])


class Blocks:
    def __init__(self, cx, cm):
        self.cx, self.cm = cx, cm
        self.wa = [cx.sb(f"wa{i}", [128, DC, 256], BF16) for i in range(4)]
        self.d_wa = [Dep() for _ in range(4)]
        self.wa_rr = 0
        self.tmp = cx.sb("tmp32", [128, 2048], F32)
        self.d_tmp = Dep()
        self.hb = cx.sb("hb_sb", [128, 2048], BF16)
        self.d_hb = Dep()
        self.ss = cx.sb("ss", [128, 4], F32)
        self.d_ss = Dep()

    def next_wa(self):
        i = self.wa_rr
        self.wa_rr = (i + 1) % 4
        return self.wa[i], self.d_wa[i]

    def alloc_sbc(self):
        cx = self.cx
        self.c_sb = cx.sb("c_sb", [128, DC], F32)
        self.s_sb = cx.sb("s_sb", [128, DC], F32)
        self.sbc = cx.sb("sbc", [128, DC, 128], BF16)
        self.d_sbc, self.d_c = Dep(), Dep()

    def make_sbc(self, cvec_ap):
        cx, ms = self.cx, self.cx.ms
        c_sb, s_sb, sbc, d_sbc, d_c = self.c_sb, self.s_sb, self.sbc, self.d_sbc, self.d_c
        ms.dma("sp", lambda e: e.dma_start(out=c_sb[:], in_=cvec_ap, allow_slow_non_contiguous=True), writes=[d_c])
        ms.op("act", lambda e: e.activation(out=s_sb[:], in_=c_sb[:], func=AF.Silu), reads=[d_c], writes=[d_c])
        ms.op("dve", lambda e: e.tensor_copy(out=sbc[:], in_=s_sb[:].unsqueeze(2).to_broadcast([128, DC, 128])),
              reads=[d_c], writes=[d_sbc])

    def mod_tile(self, out_t, d_out, sbc, d_sbc, w_ada, b_ada, chunk, kind, gvec=None):
        cx, ms, cm = self.cx, self.cx.ms, self.cm
        wv = w_ada.rearrange("(dc p) n -> p dc n", p=128)
        for cg in range(8):
            c0 = chunk * D + cg * 256
            wb, d_wb = self.next_wa()
            ms.dma("pool", lambda e, wb=wb, c0=c0: e.dma_start(out=wb[:], in_=wv[:, :, c0:c0 + 256]), writes=[d_wb])
            bank, d_bank = cm.banks[cg % 2], cm.d_bank[cg % 2]

            def mm(e, wb=wb, bank=bank):
                ins = None
                for dc in range(DC):
                    ins = e.matmul(bank[:, 0:256], lhsT=sbc[:, dc, :], rhs=wb[:, dc, :], start=(dc == 0), stop=(dc == DC - 1))
                return ins
            ms.op("pe", mm, reads=[d_wb, d_sbc], writes=[d_bank])
            bt = self.tmp
            ms.dma("sp", lambda e, c0=c0: e.dma_start(out=bt[:, 0:256], in_=bcast_row(b_ada[0], b_ada[1] + c0, 256)), writes=[self.d_tmp])
            osl = out_t[:, cg * 256:(cg + 1) * 256]
            ms.op("dve", lambda e, bank=bank, osl=osl: e.tensor_tensor(out=osl, in0=bank[:, 0:256], in1=bt[:, 0:256], op=ALU.add),
                  reads=[d_bank, self.d_tmp], writes=[d_out])
        if kind == "scale":
            gt, goff = gvec
            ms.dma("sp", lambda e: e.dma_start(out=self.tmp[:], in_=bcast_row(gt, goff, D)), writes=[self.d_tmp])
            ms.op("dve", lambda e: e.scalar_tensor_tensor(out=out_t[:], in0=out_t[:], scalar=1.0, in1=self.tmp[:], op0=ALU.add, op1=ALU.mult),
                  reads=[d_out, self.d_tmp], writes=[d_out])
        elif kind == "gate" and gvec is not None and gvec != 1.0:
            ms.op("dve", lambda e: e.tensor_scalar(out=out_t[:], in0=out_t[:], scalar1=float(gvec), scalar2=None, op0=ALU.mult),
                  reads=[d_out], writes=[d_out])

    def norm_mod_T(self, xb, d_xb, A, d_A, B, d_B, hT, d_hT, col0, T):
        cx, ms, cm = self.cx, self.cx.ms, self.cm
        ss = self.ss
        ms.op("dve", lambda e: e.memset(ss[:, 0:1], 0.0), writes=[self.d_ss])
        ms.op("act", lambda e: e.activation(out=self.tmp[:], in_=xb, func=AF.Square, accum_out=ss[:, 0:1]),
              reads=[d_xb, self.d_ss], writes=[self.d_tmp, self.d_ss])
        ms.op("act", lambda e: e.activation(out=ss[:, 1:2], in_=ss[:, 0:1], func=AF.Sqrt, scale=1.0 / D, bias=cm.eps_t[:]),
              reads=[self.d_ss, cm.d_eps], writes=[self.d_ss])
        ms.op("dve", lambda e: e.reciprocal(out=ss[:, 2:3], in_=ss[:, 1:2]), reads=[self.d_ss], writes=[self.d_ss])
        ms.op("dve", lambda e: e.scalar_tensor_tensor(out=self.tmp[:], in0=xb, scalar=ss[:, 2:3], in1=A[:], op0=ALU.mult, op1=ALU.mult),
              reads=[d_xb, self.d_ss, d_A], writes=[self.d_tmp])
        ms.op("dve", lambda e: e.tensor_tensor(out=self.hb[:], in0=self.tmp[:], in1=B[:], op=ALU.add),
              reads=[self.d_tmp, d_B], writes=[self.d_hb])
        self.transpose_to(self.hb, self.d_hb, hT, d_hT, col0, DC)

    def transpose_to(self, src, d_src, hT, d_hT, col0, nchunks):
        ms, cm = self.cx.ms, self.cm
        for half in range((nchunks + 7) // 8):
            n = min(8, nchunks - half * 8)
            bi = 2 + (half % 2)
            bank, d_bank = cm.banks[bi], cm.d_bank[bi]
            bview = bank[:].bitcast(BF16)

            def tr(e, half=half, n=n, bview=bview):
                ins = None
                for j in range(n):
                    c = half * 8 + j
                    ins = e.transpose(out=bview[:, j * 128:(j + 1) * 128], in_=src[:, c * 128:(c + 1) * 128], identity=cm.ident_b[:])
                return ins
            ms.op("pe", tr, reads=[d_src, cm.d_ident], writes=[d_bank])
            dst = hT[:, half * 8:half * 8 + n, col0:col0 + 128]
            ms.op("act", lambda e, bview=bview, dst=dst, n=n: e.activation(
                out=dst, in_=bview[:, 0:n * 128].rearrange("p (c t) -> p c t", t=128), func=AF.Copy),
                reads=[d_bank], writes=[d_hT])

    def alloc_ffn(self, T):
        cx = self.cx
        self.act = cx.sb("ffn_act", [128, FC, T], BF16)
        self.d_act = [Dep() for _ in range(FC)]
        self.alias_deps = []
        self.sg = [cx.sb(f"sg{i}", [128, T], F32) for i in range(2)]
        self.d_sg = [Dep() for _ in range(2)]

    def ffn(self, hT, d_hT, T, w_up, w_dn, xt, d_xt, G, d_G):
        cx, ms, cm = self.cx, self.cx.ms, self.cm
        NB = T // 128
        wuv = w_up.rearrange("(dc p) n -> p dc n", p=128)
        wdv = w_dn.rearrange("(fc p) n -> p fc n", p=128)
        for fp in range(FC // 2):
            wg, d_wg = self.next_wa()
            wu, d_wu = self.next_wa()
            ms.dma("pool", lambda e, wg=wg, fp=fp: e.dma_start(out=wg[:], in_=wuv[:, :, fp * 256:(fp + 1) * 256]), writes=[d_wg])
            ms.dma("pool", lambda e, wu=wu, fp=fp: e.dma_start(out=wu[:], in_=wuv[:, :, DFF + fp * 256:DFF + (fp + 1) * 256]), writes=[d_wu])
            for h in range(2):
                fc = fp * 2 + h
                pb = (fc % 2) * 2
                bg, d_bg = cm.banks[pb], cm.d_bank[pb]
                bu, d_bu = cm.banks[pb + 1], cm.d_bank[pb + 1]

                def mm(e, w=wg, bank=bg, h=h):
                    ins = None
                    for dc in range(DC):
                        ins = e.matmul(bank[:, 0:T], lhsT=w[:, dc, h * 128:(h + 1) * 128], rhs=hT[:, dc, 0:T], start=(dc == 0), stop=(dc == DC - 1))
                    return ins
                ms.op("pe", mm, reads=[d_wg, d_hT], writes=[d_bg])
                ms.op("pe", lambda e, mm=mm, h=h, wu=wu, bu=bu: mm(e, wu, bu, h), reads=[d_wu, d_hT], writes=[d_bu])
                sg, d_sg = self.sg[fc % 2], self.d_sg[fc % 2]
                ms.op("act", lambda e, sg=sg, bg=bg: e.activation(out=sg[:, 0:T], in_=bg[:, 0:T], func=AF.Silu), reads=[d_bg], writes=[d_sg])
                ms.op("dve", lambda e, sg=sg, bu=bu, fc=fc: e.tensor_tensor(out=self.act[:, fc, 0:T], in0=sg[:, 0:T], in1=bu[:, 0:T], op=ALU.mult),
                      reads=[d_sg, d_bu], writes=[self.d_act[fc]] + self.alias_deps)
        for cg in range(4):
            for q in range(6):
                nf = 8 if q < 5 else 4
                wb, d_wd = self.next_wa()
                wd = wb[:].rearrange("p a b -> p (a b)").rearrange("p (f n) -> p f n", n=512)
                ms.dma("pool", lambda e, wd=wd, q=q, cg=cg, nf=nf: e.dma_start(out=wd[:, 0:nf, :], in_=wdv[:, q * 8:q * 8 + nf, cg * 512:(cg + 1) * 512]), writes=[d_wd])

                def mmd(e, wd=wd, q=q, nf=nf):
                    ins = None
                    for i in range(nf):
                        fc = q * 8 + i
                        for tb in range(NB):
                            ins = e.matmul(cm.banks[4 + tb][:, :], lhsT=self.act[:, fc, tb * 128:(tb + 1) * 128], rhs=wd[:, i, :],
                                           start=(fc == 0), stop=(fc == FC - 1))
                    return ins
                ms.op("pe", mmd, reads=[d_wd] + self.d_act[q * 8:q * 8 + nf], writes=cm.d_bank[4:4 + NB])
            for tb in range(NB):
                sl = slice(cg * 512, (cg + 1) * 512)
                ms.op("dve", lambda e, tb=tb, sl=sl: e.tensor_tensor(out=self.tmp[:, 0:512], in0=cm.banks[4 + tb][:, :], in1=G[:, sl], op=ALU.mult),
                      reads=[cm.d_bank[4 + tb], d_G], writes=[self.d_tmp])
                ms.op("dve", lambda e, tb=tb, sl=sl: e.tensor_tensor(out=xt[:, tb, sl], in0=self.tmp[:, 0:512], in1=xt[:, tb, sl], op=ALU.add),
                      reads=[self.d_tmp, d_xt[tb]], writes=[d_xt[tb]])


def build_l1(n_lat_tiles=4, do_ctx=True):
    cx = Ctx()
    ms = cx.ms
    cm = Common(cx)
    bl = Blocks(cx, cm)
    T = 512
    NLT = n_lat_tiles
    x_in = cx.inp("x", [TOK, D])
    ctx_in = cx.inp("ctx", [CTX, D])
    c_in = cx.inp("c", [D])
    cc_in = cx.inp("c_ctx", [D])
    w_ada = cx.inp("w_ada", [D, 5 * D])
    b_ada_t = cx.inp("b_ada", [5 * D]).tensor
    g_norm_t = cx.inp("g_norm", [3 * D]).tensor
    w_up = cx.inp("w_up", [D, 2 * DFF])
    w_dn = cx.inp("w_dn", [DFF, D])
    w_in = cx.inp("w_in", [D, INW])
    qk_gain_t = cx.inp("qk_gain", [1280]).tensor
    cos_in = cx.inp("rope_cos", [TOK, 64])
    sin_in = cx.inp("rope_sin", [TOK, 64])
    x1_out, d_x1o = cx.out("x1", [TOK, D])
    qT_out, d_qTo = cx.out("qT", [128, 8, TOK])
    kT_out, d_kTo = cx.out("kT", [128, 2, TOK])
    v_out, d_vo = cx.out("v", [TOK, 256])
    kTc_out, d_kTco = cx.out("kTc", [128, 2, CTX])
    vc_out, d_vco = cx.out("vc", [CTX, 256])
    hy_out, d_hyo = cx.out("hyT", [128, 24, TOK])

    bl.alloc_sbc()
    mods = [cx.sb(f"mod{i}", [128, D], F32) for i in range(5)]
    d_mods = [Dep() for _ in range(5)]
    bl.alloc_ffn(T)
    xt = cx.sb("xt", [128, 4, D], F32)
    d_xt = [Dep() for _ in range(4)]
    hT = cx.sb("hT", [128, DC, T], BF16)
    d_hT = Dep()
    qkv = bl.act[:].rearrange("p a b -> p (a b)").bitcast(F32)[:, 0:4 * 1536].rearrange("p (t n) -> p t n", n=1536)
    d_qkv = [Dep() for _ in range(4)]
    bl.alias_deps = d_qkv
    gain = cx.sb("gain", [128, 1280], F32)
    d_gain = Dep()
    ms.dma("sp", lambda e: e.dma_start(out=gain[:], in_=bcast_row(qk_gain_t, 0, 1280)), writes=[d_gain])
    cs = cx.sb("cs", [128, 2, 64], F32)
    d_cs = Dep()
    qkb = cx.sb("qkb", [128, 1280], BF16)
    d_qkb = Dep()
    qkTf = cx.sb("qkTf", [128, 10, 128], F32)
    d_qkTf = Dep()
    hst = [cx.sb(f"hst{i}", [128, T], F32) for i in range(2)]
    d_hst = [Dep() for _ in range(2)]
    rs = cx.sb("rs", [128, 3, 10], F32)
    d_rs = Dep()
    rt = [cx.sb(f"rt{i}", [128, 10, 64], F32) for i in range(2)]
    d_rt = [Dep() for _ in range(2)]

    def set_mods(sbc, d_sbc):
        bl.mod_tile(mods[0], d_mods[0], sbc, d_sbc, w_ada, (b_ada_t, 0), 1, "scale", (g_norm_t, 0))
        bl.mod_tile(mods[1], d_mods[1], sbc, d_sbc, w_ada, (b_ada_t, 0), 0, "shift")
        bl.mod_tile(mods[2], d_mods[2], sbc, d_sbc, w_ada, (b_ada_t, 0), 2, "gate", 0.5)
        bl.mod_tile(mods[3], d_mods[3], sbc, d_sbc, w_ada, (b_ada_t, 0), 4, "scale", (g_norm_t, D))
        bl.mod_tile(mods[4], d_mods[4], sbc, d_sbc, w_ada, (b_ada_t, 0), 3, "shift")

    wiv = w_in.rearrange("(dc p) n -> p dc n", p=128)

    def tile(src, tok0, Tt, is_ctx):
        NB = Tt // 128
        for tb in range(NB):
            ms.dma("sp", lambda e, tb=tb: e.dma_start(out=xt[:, tb, :], in_=src[tok0 + tb * 128: tok0 + (tb + 1) * 128, :]), writes=[d_xt[tb]])
        for tb in range(NB):
            bl.norm_mod_T(xt[:, tb, :], d_xt[tb], mods[0], d_mods[0], mods[1], d_mods[1], hT, d_hT, tb * 128, Tt)
        bl.ffn(hT, d_hT, Tt, w_up, w_dn, xt, d_xt, mods[2], d_mods[2])
        if not is_ctx:
            for tb in range(NB):
                ms.dma("sp", lambda e, tb=tb: e.dma_start(out=x1_out[tok0 + tb * 128: tok0 + (tb + 1) * 128, :], in_=xt[:, tb, :]),
                       reads=[d_xt[tb]], writes=[d_x1o])
        for tb in range(NB):
            bl.norm_mod_T(xt[:, tb, :], d_xt[tb], mods[3], d_mods[3], mods[4], d_mods[4], hT, d_hT, tb * 128, Tt)
        for cg6 in (range(4, 6) if is_ctx else range(6)):
            wb, d_wb = bl.next_wa()
            ms.dma("pool", lambda e, wb=wb, cg6=cg6: e.dma_start(out=wb[:], in_=wiv[:, :, cg6 * 256:(cg6 + 1) * 256]), writes=[d_wb])
            for tb in range(NB):
                bi = tb % 2
                bank, d_bank = cm.banks[bi], cm.d_bank[bi]

                def mm(e, wb=wb, bank=bank, tb=tb):
                    ins = None
                    for dc in range(DC):
                        ins = e.matmul(bank[:, 0:256], lhsT=hT[:, dc, tb * 128:(tb + 1) * 128], rhs=wb[:, dc, :], start=(dc == 0), stop=(dc == DC - 1))
                    return ins
                ms.op("pe", mm, reads=[d_wb, d_hT], writes=[d_bank])
                ms.op("act", lambda e, bank=bank, tb=tb, cg6=cg6: e.activation(out=qkv[:, tb, cg6 * 256:(cg6 + 1) * 256], in_=bank[:, 0:256], func=AF.Copy),
                      reads=[d_bank], writes=[d_qkv[tb]] + bl.d_act)
        for tb in range(NB):
            t0 = tok0 + tb * 128
            h0, nh = (8, 2) if is_ctx else (0, 10)
            xin = qkv[:, tb, h0 * 128:(h0 + nh) * 128]
            x3 = xin.rearrange("p (h d) -> p h d", d=128)
            tm = bl.tmp[:, 0:nh * 128]
            tm3 = tm.rearrange("p (h d) -> p h d", d=128)
            ms.op("act", lambda e, xin=xin, tm=tm: e.activation(out=tm, in_=xin, func=AF.Square), reads=[d_qkv[tb]], writes=[bl.d_tmp])
            ms.op("dve", lambda e, tm3=tm3, nh=nh: e.tensor_reduce(out=rs[:, 0, 0:nh], in_=tm3, axis=AX.X, op=ALU.add), reads=[bl.d_tmp], writes=[d_rs])
            ms.op("act", lambda e, nh=nh: e.activation(out=rs[:, 1, 0:nh], in_=rs[:, 0, 0:nh], func=AF.Sqrt, scale=1.0 / HD, bias=cm.eps_t[:]),
                  reads=[d_rs, cm.d_eps], writes=[d_rs])
            ms.op("dve", lambda e, nh=nh: e.reciprocal(out=rs[:, 2, 0:nh], in_=rs[:, 1, 0:nh]), reads=[d_rs], writes=[d_rs])
            ms.op("dve", lambda e, x3=x3, tm3=tm3, nh=nh: e.tensor_tensor(out=tm3, in0=x3, in1=rs[:, 2, 0:nh].unsqueeze(2).to_broadcast([128, nh, 128]), op=ALU.mult),
                  reads=[d_qkv[tb], d_rs], writes=[bl.d_tmp])
            if is_ctx:
                ms.op("dve", lambda e, tm=tm, h0=h0, nh=nh: e.tensor_tensor(out=qkb[:, 0:nh * 128], in0=tm, in1=gain[:, h0 * 128:(h0 + nh) * 128], op=ALU.mult),
                      reads=[bl.d_tmp, d_gain], writes=[d_qkb])
            else:
                ms.op("dve", lambda e, tm=tm, h0=h0, nh=nh: e.tensor_tensor(out=tm, in0=tm, in1=gain[:, h0 * 128:(h0 + nh) * 128], op=ALU.mult),
                      reads=[bl.d_tmp, d_gain], writes=[bl.d_tmp])
                ms.dma("sp", lambda e, t0=t0: e.dma_start(out=cs[:, 0, :], in_=cos_in[t0:t0 + 128, :]), writes=[d_cs])
                ms.dma("sp", lambda e, t0=t0: e.dma_start(out=cs[:, 1, :], in_=sin_in[t0:t0 + 128, :]), writes=[d_cs])
                tm4 = tm.rearrange("p (h i two) -> p h i two", i=64, two=2)
                ob4 = qkb[:, 0:nh * 128].rearrange("p (h i two) -> p h i two", i=64, two=2)
                cb = cs[:, 0, :].unsqueeze(1).to_broadcast([128, nh, 64])
                sb_ = cs[:, 1, :].unsqueeze(1).to_broadcast([128, nh, 64])
                x0, x1 = tm4[:, :, :, 0], tm4[:, :, :, 1]
                ms.op("dve", lambda e, x0=x0, cb=cb: e.tensor_tensor(out=rt[0][:], in0=x0, in1=cb, op=ALU.mult), reads=[bl.d_tmp, d_cs], writes=[d_rt[0]])
                ms.op("dve", lambda e, x1=x1, sb_=sb_: e.tensor_tensor(out=rt[1][:], in0=x1, in1=sb_, op=ALU.mult), reads=[bl.d_tmp, d_cs], writes=[d_rt[1]])
                ms.op("dve", lambda e, ob4=ob4: e.tensor_tensor(out=ob4[:, :, :, 0], in0=rt[0][:], in1=rt[1][:], op=ALU.subtract), reads=[d_rt[0], d_rt[1]], writes=[d_qkb])
                ms.op("dve", lambda e, x0=x0, sb_=sb_: e.tensor_tensor(out=rt[0][:], in0=x0, in1=sb_, op=ALU.mult), reads=[bl.d_tmp, d_cs], writes=[d_rt[0]])
                ms.op("dve", lambda e, x1=x1, cb=cb: e.tensor_tensor(out=rt[1][:], in0=x1, in1=cb, op=ALU.mult), reads=[bl.d_tmp, d_cs], writes=[d_rt[1]])
                ms.op("dve", lambda e, ob4=ob4: e.tensor_tensor(out=ob4[:, :, :, 1], in0=rt[0][:], in1=rt[1][:], op=ALU.add), reads=[d_rt[0], d_rt[1]], writes=[d_qkb])
            bl.transpose_to(qkb, d_qkb, qkTf, d_qkTf, 0, nh)
            if is_ctx:
                ms.dma("sp", lambda e, t0=t0: e.dma_start(out=kTc_out[:, :, t0:t0 + 128], in_=qkTf[:, 0:2, :]), reads=[d_qkTf], writes=[d_kTco])
                ms.dma("sp", lambda e, t0=t0, tb=tb: e.dma_start(out=vc_out[t0:t0 + 128, :], in_=qkv[:, tb, 1280:1536]), reads=[d_qkv[tb]], writes=[d_vco])
            else:
                ms.dma("sp", lambda e, t0=t0: e.dma_start(out=qT_out[:, :, t0:t0 + 128], in_=qkTf[:, 0:8, :]), reads=[d_qkTf], writes=[d_qTo])
                ms.dma("sp", lambda e, t0=t0: e.dma_start(out=kT_out[:, :, t0:t0 + 128], in_=qkTf[:, 8:10, :]), reads=[d_qkTf], writes=[d_kTo])
                ms.dma("sp", lambda e, t0=t0, tb=tb: e.dma_start(out=v_out[t0:t0 + 128, :], in_=qkv[:, tb, 1280:1536]), reads=[d_qkv[tb]], writes=[d_vo])
        if not is_ctx:
            for cp in range(12):
                wb, d_wb = bl.next_wa()
                ms.dma("pool", lambda e, wb=wb, cp=cp: e.dma_start(out=wb[:], in_=wiv[:, :, 1536 + cp * 256:1536 + (cp + 1) * 256]), writes=[d_wb])
                for h in range(2):
                    ch = cp * 2 + h
                    bi = ch % 2
                    bank, d_bank = cm.banks[bi], cm.d_bank[bi]

                    def mmh(e, wb=wb, bank=bank, h=h):
                        ins = None
                        for dc in range(DC):
                            ins = e.matmul(bank[:, 0:Tt], lhsT=wb[:, dc, h * 128:(h + 1) * 128], rhs=hT[:, dc, 0:Tt], start=(dc == 0), stop=(dc == DC - 1))
                        return ins
                    ms.op("pe", mmh, reads=[d_wb, d_hT], writes=[d_bank])
                    st_, d_st = hst[ch % 2], d_hst[ch % 2]
                    ms.op("act", lambda e, st_=st_, bank=bank: e.activation(out=st_[:, 0:Tt], in_=bank[:, 0:Tt], func=AF.Copy), reads=[d_bank], writes=[d_st])
                    ms.dma("sp", lambda e, st_=st_, ch=ch: e.dma_start(out=hy_out[:, ch, tok0:tok0 + Tt], in_=st_[:, 0:Tt]), reads=[d_st], writes=[d_hyo])

    if do_ctx:
        bl.make_sbc(cc_in.rearrange("(dc p) -> p dc", p=128))
        set_mods(bl.sbc, bl.d_sbc)
        tile(ctx_in, 0, CTX, True)
    bl.make_sbc(c_in.rearrange("(dc p) -> p dc", p=128))
    set_mods(bl.sbc, bl.d_sbc)
    for lt in range(NLT):
        tile(x_in, lt * T, T, False)
    cx.finish()
    return cx


def _rope_tables():
    t = np.arange(SEQ)
    inv = (10000.0 ** (-np.arange(0, 64, 2, dtype=np.float32) / 64.0)).astype(np.float32)
    ang = np.concatenate([(t // 64).astype(np.float32)[:, None] * inv, (t % 64).astype(np.float32)[:, None] * inv], axis=-1)
    return np.cos(ang).astype(np.float32), np.sin(ang).astype(np.float32)


def l1_in_maps(inp):
    cos, sin = _rope_tables()
    ident = np.eye(128, dtype=np.float32)
    qk_gain = np.ascontiguousarray(np.concatenate([np.tile(inp["q_norm"][0], 8), np.tile(inp["k_norm"][0], 2)]).astype(np.float32))
    shared = dict(
        ident=ident, c_ctx=np.ascontiguousarray(inp["c_ctx"]), w_ada=np.ascontiguousarray(inp["w_ada"][0][:, :5 * D]),
        b_ada=np.ascontiguousarray(inp["b_ada"][0][:5 * D]), g_norm=np.ascontiguousarray(inp["g_norm"][0].reshape(-1)),
        w_up=np.ascontiguousarray(inp["w_ffn1_up"][0]), w_dn=np.ascontiguousarray(inp["w_ffn1_down"][0]),
        w_in=np.ascontiguousarray(inp["w_in"][0]), qk_gain=qk_gain)
    maps = []
    for core in range(NCORE):
        b, tq = core // 4, core % 4
        m = dict(shared)
        m["x"] = np.ascontiguousarray(inp["x"][b, tq * TOK:(tq + 1) * TOK])
        m["ctx"] = np.ascontiguousarray(inp["ctx"][b])
        m["c"] = np.ascontiguousarray(inp["c"][b])
        m["rope_cos"] = np.ascontiguousarray(cos[tq * TOK:(tq + 1) * TOK])
        m["rope_sin"] = np.ascontiguousarray(sin[tq * TOK:(tq + 1) * TOK])
        maps.append(m)
    return maps


def build_l3(n_tiles=4):
    cx = Ctx()
    ms = cx.ms
    cm = Common(cx)
    bl = Blocks(cx, cm)
    T = 512
    x1_in = cx.inp("x1", [TOK, D])
    at_in = cx.inp("attnT", [128, 8, TOK])
    hy_in = cx.inp("hyoT", [128, 8, TOK])
    c_in = cx.inp("c", [D])
    w_ada = cx.inp("w_ada", [D, 4 * D])
    b_ada_t = cx.inp("b_ada", [4 * D]).tensor
    g_norm_t = cx.inp("g_norm", [3 * D]).tensor
    g_out_in = cx.inp("g_out", [D])
    w_out = cx.inp("w_out", [D, D])
    w_up = cx.inp("w_up", [D, 2 * DFF])
    w_dn = cx.inp("w_dn", [DFF, D])
    y_out, d_yo = cx.out("out", [TOK, D])

    bl.alloc_sbc()
    mods = [cx.sb(f"mod{i}", [128, D], F32) for i in range(4)]
    d_mods = [Dep() for _ in range(4)]
    bl.alloc_ffn(T)
    xt = cx.sb("xt", [128, 4, D], F32)
    d_xt = [Dep() for _ in range(4)]
    hT = cx.sb("hT", [128, DC, T], BF16)
    d_hT = Dep()
    yT = bl.act[:].rearrange("p a b -> p (a b)").bitcast(F32)[:, 0:16 * T].rearrange("p (c t) -> p c t", t=T)
    d_yT = Dep()
    bl.alias_deps = [d_yT]
    gcol = cx.sb("gcol", [128, DC], F32)
    d_gcol = Dep()
    ms.dma("sp", lambda e: e.dma_start(out=gcol[:], in_=g_out_in.rearrange("(c p) -> p c", p=128), allow_slow_non_contiguous=True), writes=[d_gcol])
    ones_b = cx.sb("ones_b", [128, 128], BF16)
    d_ones = Dep()
    ms.op("dve", lambda e: e.memset(ones_b[:], 1.0), writes=[d_ones])
    sqb = [cx.sb(f"sqb{i}", [128, T], BF16) for i in range(2)]
    d_sqb = [Dep() for _ in range(2)]
    rstd = cx.sb("rstd", [128, 2, T], F32)
    d_rstd = Dep()

    bl.make_sbc(c_in.rearrange("(dc p) -> p dc", p=128))
    bl.mod_tile(mods[0], d_mods[0], bl.sbc, bl.d_sbc, w_ada, (b_ada_t, 0), 0, "gate", 1.0)
    bl.mod_tile(mods[1], d_mods[1], bl.sbc, bl.d_sbc, w_ada, (b_ada_t, 0), 2, "scale", (g_norm_t, 2 * D))
    bl.mod_tile(mods[2], d_mods[2], bl.sbc, bl.d_sbc, w_ada, (b_ada_t, 0), 1, "shift")
    bl.mod_tile(mods[3], d_mods[3], bl.sbc, bl.d_sbc, w_ada, (b_ada_t, 0), 3, "gate", 0.5)
    wov = w_out.rearrange("(c p) n -> p c n", p=128)

    def tile3(tok0):
        for tb in range(4):
            ms.dma("sp", lambda e, tb=tb: e.dma_start(out=xt[:, tb, :], in_=x1_in[tok0 + tb * 128: tok0 + (tb + 1) * 128, :]), writes=[d_xt[tb]])
        ms.dma("sp", lambda e: e.dma_start(out=yT[:, 0:8, :], in_=at_in[:, :, tok0:tok0 + T]), writes=[d_yT] + bl.d_act)
        ms.dma("sp", lambda e: e.dma_start(out=yT[:, 8:16, :], in_=hy_in[:, :, tok0:tok0 + T]), writes=[d_yT] + bl.d_act)
        for g in range(2):
            bank, d_bank = cm.banks[g], cm.d_bank[g]
            for c8 in range(8):
                c = g * 8 + c8
                sq, d_sq = sqb[c % 2], d_sqb[c % 2]
                ms.op("act", lambda e, sq=sq, c=c: e.activation(out=sq[:], in_=yT[:, c, :], func=AF.Square), reads=[d_yT], writes=[d_sq])
                ms.op("pe", lambda e, sq=sq, bank=bank, c8=c8: e.matmul(bank[:, 0:T], lhsT=ones_b[:], rhs=sq[:], start=(c8 == 0), stop=(c8 == 7)),
                      reads=[d_sq, d_ones], writes=[d_bank])
            ms.op("act", lambda e, bank=bank, g=g: e.activation(out=rstd[:, g, :], in_=bank[:, 0:T], func=AF.Sqrt, scale=1.0 / 1024, bias=cm.eps_t[:]),
                  reads=[d_bank, cm.d_eps], writes=[d_rstd])
            ms.op("dve", lambda e, g=g: e.reciprocal(out=rstd[:, g, :], in_=rstd[:, g, :]), reads=[d_rstd], writes=[d_rstd])
        for c in range(16):
            ms.op("dve", lambda e, c=c: e.scalar_tensor_tensor(out=hT[:, c, :], in0=yT[:, c, :], scalar=gcol[:, c:c + 1], in1=rstd[:, c // 8, :],
                                                              op0=ALU.mult, op1=ALU.mult),
                  reads=[d_yT, d_gcol, d_rstd], writes=[d_hT])
        for cg in range(8):
            wb, d_wb = bl.next_wa()
            ms.dma("pool", lambda e, wb=wb, cg=cg: e.dma_start(out=wb[:], in_=wov[:, :, cg * 256:(cg + 1) * 256]), writes=[d_wb])
            for tb in range(4):
                bi = tb % 2
                bank, d_bank = cm.banks[bi], cm.d_bank[bi]

                def mm(e, wb=wb, bank=bank, tb=tb):
                    ins = None
                    for c in range(DC):
                        ins = e.matmul(bank[:, 0:256], lhsT=hT[:, c, tb * 128:(tb + 1) * 128], rhs=wb[:, c, :], start=(c == 0), stop=(c == DC - 1))
                    return ins
                ms.op("pe", mm, reads=[d_wb, d_hT], writes=[d_bank])
                sl = slice(cg * 256, (cg + 1) * 256)
                ms.op("dve", lambda e, bank=bank, sl=sl: e.tensor_tensor(out=bl.tmp[:, 0:256], in0=bank[:, 0:256], in1=mods[0][:, sl], op=ALU.mult),
                      reads=[d_bank, d_mods[0]], writes=[bl.d_tmp])
                ms.op("dve", lambda e, tb=tb, sl=sl: e.tensor_tensor(out=xt[:, tb, sl], in0=bl.tmp[:, 0:256], in1=xt[:, tb, sl], op=ALU.add),
                      reads=[bl.d_tmp, d_xt[tb]], writes=[d_xt[tb]])
        for tb in range(4):
            bl.norm_mod_T(xt[:, tb, :], d_xt[tb], mods[1], d_mods[1], mods[2], d_mods[2], hT, d_hT, tb * 128, T)
        bl.ffn(hT, d_hT, T, w_up, w_dn, xt, d_xt, mods[3], d_mods[3])
        for tb in range(4):
            ms.dma("sp", lambda e, tb=tb: e.dma_start(out=y_out[tok0 + tb * 128: tok0 + (tb + 1) * 128, :], in_=xt[:, tb, :]),
                   reads=[d_xt[tb]], writes=[d_yo])

    for lt in range(n_tiles):
        tile3(lt * T)
    cx.finish()
    return cx


def l3_in_maps(inp, x1s, attnTs, hyoTs):
    ident = np.eye(128, dtype=np.float32)
    shared = dict(
        ident=ident, w_ada=np.ascontiguousarray(inp["w_ada"][0][:, 5 * D:]), b_ada=np.ascontiguousarray(inp["b_ada"][0][5 * D:]),
        g_norm=np.ascontiguousarray(inp["g_norm"][0].reshape(-1)), g_out=np.ascontiguousarray(inp["g_out"][0]),
        w_out=np.ascontiguousarray(inp["w_out"][0]), w_up=np.ascontiguousarray(inp["w_ffn2_up"][0]),
        w_dn=np.ascontiguousarray(inp["w_ffn2_down"][0]))
    maps = []
    for core in range(NCORE):
        m = dict(shared)
        m["x1"] = x1s[core]
        m["attnT"] = attnTs[core]
        m["hyoT"] = hyoTs[core]
        m["c"] = np.ascontiguousarray(inp["c"][core // 4])
        maps.append(m)
    return maps


NKEY = CTX + SEQ
NKC = NKEY // 128


def build_l2a(n_heads=8, n_qt=4):
    cx = Ctx()
    ms = cx.ms
    cm = Common(cx)
    q_in = cx.inp("qT", [128, 8, TOK])
    k_in = cx.inp("kT", [128, 2, NKEY])
    v_in = cx.inp("v", [NKEY, 256])
    o_out, d_oo = cx.out("attnT", [128, 8, TOK])
    q_b = cx.sb("q_b", [128, 8, TOK], BF16)
    k_b = cx.sb("k_b", [128, 2, NKEY], BF16)
    v_b = cx.sb("v_b", [128, NKC, 256], BF16)
    d_q = [Dep() for _ in range(8)]
    d_k = [Dep() for _ in range(2)]
    d_v = Dep()
    vv = v_in.rearrange("(kc p) n -> p kc n", p=128)
    for kv in range(2):
        for half in range(2):
            sl = slice(half * (NKEY // 2), (half + 1) * (NKEY // 2))
            ms.dma("pool", lambda e, kv=kv, sl=sl: e.dma_start(out=k_b[:, kv, sl], in_=k_in[:, kv, sl]), writes=[d_k[kv]])
    for part in range(3):
        sl = slice(part * 22, (part + 1) * 22)
        ms.dma("pool", lambda e, sl=sl: e.dma_start(out=v_b[:, sl, :], in_=vv[:, sl, :]), writes=[d_v])
    for h in range(8):
        ms.dma("pool", lambda e, h=h: e.dma_start(out=q_b[:, h, :], in_=q_in[:, h, :]), writes=[d_q[h]])
    ones_b = cx.sb("ones_b", [128, 128], BF16)
    d_ones = Dep()
    ms.op("dve", lambda e: e.memset(ones_b[:], 1.0), writes=[d_ones])
    NP = 4
    P = [cx.sb(f"P{i}", [128, 512], BF16) for i in range(NP)]
    d_P = [Dep() for _ in range(NP)]
    rden = cx.sb("rden", [128, 512], F32)
    d_rden = Dep()
    ost = [cx.sb(f"ost{i}", [128, 512], F32) for i in range(2)]
    d_ost = [Dep() for _ in range(2)]
    LA = 2
    scale = float(HD) ** -0.5
    gi = 0
    step_id = 0
    for h in range(n_heads):
        kv = h // 4
        for qt in range(n_qt):
            qs = slice(qt * 512, (qt + 1) * 512)
            bO, bD = cm.banks[4 + 2 * (gi % 2)], cm.banks[5 + 2 * (gi % 2)]
            d_bO, d_bD = cm.d_bank[4 + 2 * (gi % 2)], cm.d_bank[5 + 2 * (gi % 2)]
            slots = {}
            for step in range(NKC + LA):
                if step < NKC:
                    kc = step
                    i = step_id % NP
                    step_id += 1
                    slots[kc] = i
                    sb_, d_sb = cm.banks[i], cm.d_bank[i]
                    ms.op("pe", lambda e, sb_=sb_, kc=kc, kv=kv, h=h, qs=qs: e.matmul(sb_[:, :], lhsT=k_b[:, kv, kc * 128:(kc + 1) * 128], rhs=q_b[:, h, qs], start=True, stop=True),
                          reads=[d_k[kv], d_q[h]], writes=[d_sb])
                    ms.op("act", lambda e, sb_=sb_, i=i: e.activation(out=P[i][:], in_=sb_[:, :], func=AF.Exp, scale=scale),
                          reads=[d_sb], writes=[d_P[i]])
                if step >= LA:
                    kc = step - LA
                    i = slots[kc]

                    def pv(e, kc=kc, i=i, kv=kv, bO=bO, bD=bD):
                        e.matmul(bO[:, :], lhsT=v_b[:, kc, kv * 128:(kv + 1) * 128], rhs=P[i][:], start=(kc == 0), stop=(kc == NKC - 1))
                        return e.matmul(bD[:, :], lhsT=ones_b[:], rhs=P[i][:], start=(kc == 0), stop=(kc == NKC - 1))
                    ms.op("pe", pv, reads=[d_P[i], d_v, d_ones], writes=[d_bO, d_bD])
            ms.op("dve", lambda e, bD=bD: e.reciprocal(out=rden[:], in_=bD[:, :]), reads=[d_bD], writes=[d_rden])
            o_, d_o = ost[gi % 2], d_ost[gi % 2]
            ms.op("dve", lambda e, bO=bO, o_=o_: e.tensor_tensor(out=o_[:], in0=bO[:, :], in1=rden[:], op=ALU.mult), reads=[d_bO, d_rden], writes=[d_o])
            ms.dma("sp", lambda e, o_=o_, h=h, qs=qs: e.dma_start(out=o_out[:, h, qs], in_=o_[:]), reads=[d_o], writes=[d_oo])
            gi += 1
    cx.finish()
    return cx


NLAG = 2 * SEQ
KW = 127 * 128


def build_l2h(n_ch=128):
    cx = Ctx()
    ms, nc = cx.ms, cx.nc
    cm = Common(cx)
    hy_in = cx.inp("hy3", [128, 3, 2, SEQ])
    cw_in = cx.inp("cw", [128, 3, 4])
    hb_in = cx.inp("hb", [128, 1])
    w1_in = cx.inp("fw1", [33, 64])
    w2_in = cx.inp("fw2", [64, 64])
    w3_in = cx.inp("fw3", [64, 64])
    w4_in = cx.inp("fw4", [64, 2, 128])
    fb_in = cx.inp("fb", [64, 4])
    dec_in = cx.inp("dec", [128, 2])
    zt_in = cx.inp("zt", [33, NLAG])
    tt_t = cx.inp("tt", [NLAG]).tensor
    J_in = cx.inp("Jm", [128, 128])
    o_out, d_oo = cx.out("hyo", [128, 2, SEQ])
    kk_d = nc.dram_tensor("kk_d", [128, NLAG], BF16)
    vx_d = nc.dram_tensor("vx_d", [128, 2, SEQ], F32)
    d_kkd, d_vxd = Dep(), Dep()

    big = cx.sb("big", [128, NLAG], F32)
    d_big = Dep()
    w1 = cx.sb("w1_sb", [33, 64], F32)
    w2 = cx.sb("w2_sb", [64, 64], F32)
    w3 = cx.sb("w3_sb", [64, 64], F32)
    w4 = cx.sb("w4_sb", [64, 2, 128], F32)
    fb = cx.sb("fb_sb", [64, 8], F32)
    dec = cx.sb("dec_sb", [128, 4], F32)
    cw = cx.sb("cw_sb", [128, 3, 4], F32)
    hb = cx.sb("hb_sb", [128, 1], F32)
    Jb = cx.sb("Jb", [128, 128], BF16)
    d_w = Dep()
    for dst, src in ((w1, w1_in), (w2, w2_in), (w3, w3_in)):
        ms.dma("sp", lambda e, dst=dst, src=src: e.dma_start(out=dst[:], in_=src[:, :]), writes=[d_w])
    ms.dma("sp", lambda e: e.dma_start(out=w4[:], in_=w4_in[:, :, :]), writes=[d_w])
    ms.dma("sp", lambda e: e.dma_start(out=fb[:, 0:4], in_=fb_in[:, :]), writes=[d_w])
    ms.dma("sp", lambda e: e.dma_start(out=dec[:, 0:2], in_=dec_in[:, :]), writes=[d_w])
    ms.dma("sp", lambda e: e.dma_start(out=cw[:], in_=cw_in[:, :, :]), writes=[d_w])
    ms.dma("sp", lambda e: e.dma_start(out=hb[:], in_=hb_in[:, :]), writes=[d_w])
    ms.dma("pool", lambda e: e.dma_start(out=Jb[:], in_=J_in[:, :]), writes=[d_w])
    ms.op("dve", lambda e: e.tensor_scalar(out=fb[:, 4:5], in0=fb[:, 3:4], scalar1=1.0 / (2 * math.pi), scalar2=None, op0=ALU.mult), reads=[d_w], writes=[d_w])
    ms.op("act", lambda e: e.activation(out=dec[:, 2:4], in_=dec[:, 0:2], func=AF.Abs), reads=[d_w], writes=[d_w])
    ms.op("dve", lambda e: e.tensor_scalar(out=dec[:, 2:4], in0=dec[:, 2:4], scalar1=-1.0, scalar2=None, op0=ALU.mult), reads=[d_w], writes=[d_w])
    FT = 512
    zs = [cx.sb(f"zs{i}", [33, FT], F32) for i in range(2)]
    d_zs = [Dep() for _ in range(2)]
    u_t = cx.sb("u_t", [64, FT], F32)
    ni_t = cx.sb("ni_t", [64, FT], I32)
    nf_t = cx.sb("nf_t", [64, FT], F32)
    hs = [cx.sb(f"hs{i}", [64, FT], F32) for i in range(2)]
    d_u, d_ni, d_nf = Dep(), Dep(), Dep()
    d_hs = [Dep() for _ in range(2)]
    ttb = cx.sb("ttb", [128, FT], F32)
    win = cx.sb("win", [128, FT], F32)
    d_ttb, d_win = Dep(), Dep()
    nrm = cx.sb("nrm", [128, 40], F32)
    d_nrm = Dep()
    ms.op("dve", lambda e: e.memset(nrm[:], 0.0), writes=[d_nrm])
    NFT = NLAG // FT
    for ti in range(NFT):
        c0 = ti * FT
        direc = 1 if ti < NFT // 2 else 0
        z, d_z = zs[ti % 2], d_zs[ti % 2]
        ms.dma("sp", lambda e, z=z, c0=c0: e.dma_start(out=z[:], in_=zt_in[:, c0:c0 + FT]), writes=[d_z])
        src, d_src = z, d_z
        for layer, wl in enumerate((w1, w2, w3)):
            bank, d_bank = cm.banks[layer % 2], cm.d_bank[layer % 2]
            K = 33 if layer == 0 else 64
            ms.op("pe", lambda e, bank=bank, wl=wl, src=src, K=K: e.matmul(bank[0:64, 0:FT], lhsT=wl[0:K, :], rhs=src[0:K, :], start=True, stop=True),
                  reads=[d_w, d_src], writes=[d_bank])
            ms.op("dve", lambda e, bank=bank, layer=layer: e.tensor_scalar(out=u_t[:], in0=bank[0:64, 0:FT], scalar1=fb[:, layer:layer + 1], scalar2=fb[:, 4:5],
                                                                          op0=ALU.add, op1=ALU.mult), reads=[d_bank, d_w], writes=[d_u])
            ms.op("dve", lambda e: e.tensor_copy(out=ni_t[:], in_=u_t[:]), reads=[d_u], writes=[d_ni])
            ms.op("dve", lambda e: e.tensor_copy(out=nf_t[:], in_=ni_t[:]), reads=[d_ni], writes=[d_nf])
            ms.op("dve", lambda e: e.tensor_tensor(out=u_t[:], in0=u_t[:], in1=nf_t[:], op=ALU.subtract), reads=[d_u, d_nf], writes=[d_u])
            h_, d_h = hs[layer % 2], d_hs[layer % 2]
            ms.op("act", lambda e, h_=h_: e.activation(out=h_[:], in_=u_t[:], func=AF.Sin, scale=2 * math.pi), reads=[d_u], writes=[d_h])
            src, d_src = h_, d_h
        bank, d_bank = cm.banks[2 + ti % 2], cm.d_bank[2 + ti % 2]
        ms.op("pe", lambda e, bank=bank, src=src, direc=direc: e.matmul(bank[:, 0:FT], lhsT=w4[:, direc, :], rhs=src[:, :], start=True, stop=True),
              reads=[d_w, d_src], writes=[d_bank])
        ms.dma("sp", lambda e, c0=c0: e.dma_start(out=ttb[:], in_=bcast_row(tt_t, c0, FT)), writes=[d_ttb])
        ms.op("act", lambda e, direc=direc: e.activation(out=win[:], in_=ttb[:], func=AF.Exp, scale=dec[:, 2 + direc:3 + direc]), reads=[d_ttb, d_w], writes=[d_win])
        ms.op("dve", lambda e, bank=bank, c0=c0: e.tensor_tensor(out=big[:, c0:c0 + FT], in0=bank[:, 0:FT], in1=win[:], op=ALU.mult),
              reads=[d_bank, d_win], writes=[d_big])
        ms.op("act", lambda e, c0=c0, ti=ti: e.activation(out=win[:], in_=big[:, c0:c0 + FT], func=AF.Abs, accum_out=nrm[:, ti:ti + 1]),
              reads=[d_big, d_nrm], writes=[d_win, d_nrm])
    ms.op("dve", lambda e: e.tensor_reduce(out=nrm[:, 32:33], in_=nrm[:, 0:32], axis=AX.X, op=ALU.add), reads=[d_nrm], writes=[d_nrm])
    ms.op("dve", lambda e: e.reciprocal(out=nrm[:, 33:34], in_=nrm[:, 32:33]), reads=[d_nrm], writes=[d_nrm])
    kkb = cx.sb("kkb", [128, 1024], BF16)
    d_kkb = Dep()
    for q in range(NLAG // 1024):
        ms.op("dve", lambda e, q=q: e.tensor_scalar(out=kkb[:], in0=big[:, q * 1024:(q + 1) * 1024], scalar1=nrm[:, 33:34], scalar2=None, op0=ALU.mult),
              reads=[d_big, d_nrm], writes=[d_kkb])
        ms.dma("sp", lambda e, q=q: e.dma_start(out=kk_d.ap()[:, q * 1024:(q + 1) * 1024], in_=kkb[:]), reads=[d_kkb], writes=[d_kkd])

    CH = 1024
    hyc = [cx.sb(f"hyc{i}", [128, CH + 2], F32) for i in range(2)]
    d_hyc = [Dep() for _ in range(2)]
    uc = [cx.sb(f"uc{i}", [128, CH], F32) for i in range(2)]
    d_uc = [Dep() for _ in range(2)]
    vx = cx.sb("vx", [128, CH], F32)
    d_vx = Dep()
    vxb = cx.sb("vxb", [128, CH], BF16)
    d_vxb = Dep()
    vtt = cx.sb("vtt", [128, CH], BF16)
    d_vtt = Dep()
    V_rev = cx.sb("V_rev", [128, 128, 128], BF16)
    d_Vrev = Dep()

    def conv(part, b, t0, k):
        h_, d_h = hyc[k], d_hyc[k]
        lo, hi = max(t0 - 1, 0), min(t0 + CH + 1, SEQ)
        if t0 == 0:
            ms.op("dve", lambda e, h_=h_: e.memset(h_[:, 0:1], 0.0), writes=[d_h])
        if t0 + CH == SEQ:
            ms.op("dve", lambda e, h_=h_: e.memset(h_[:, CH + 1:CH + 2], 0.0), writes=[d_h])
        ms.dma("sp", lambda e, h_=h_: e.dma_start(out=h_[:, lo - (t0 - 1):hi - (t0 - 1)], in_=hy_in[:, part, b, lo:hi]), writes=[d_h])
        o_, d_o = uc[k], d_uc[k]
        ms.op("act", lambda e: e.activation(out=o_[:], in_=h_[:, 1:CH + 1], func=AF.Identity, scale=cw[:, part, 1:2], bias=cw[:, part, 3:4]),
              reads=[d_h, d_w], writes=[d_o])
        ms.op("dve", lambda e: e.scalar_tensor_tensor(out=o_[:], in0=h_[:, 0:CH], scalar=cw[:, part, 0:1], in1=o_[:], op0=ALU.mult, op1=ALU.add),
              reads=[d_h, d_w, d_o], writes=[d_o])
        ms.op("dve", lambda e: e.scalar_tensor_tensor(out=o_[:], in0=h_[:, 2:CH + 2], scalar=cw[:, part, 2:3], in1=o_[:], op0=ALU.mult, op1=ALU.add),
              reads=[d_h, d_w, d_o], writes=[d_o])

    NBK = CH // 128
    for b in range(2):
        for ck in range(SEQ // CH):
            t0 = ck * CH
            conv(1, b, t0, 0)
            conv(2, b, t0, 1)
            ms.op("dve", lambda e: e.tensor_tensor(out=vx[:], in0=uc[0][:], in1=uc[1][:], op=ALU.mult), reads=[d_uc[0], d_uc[1]], writes=[d_vx])
            ms.dma("sp", lambda e, b=b, t0=t0: e.dma_start(out=vx_d.ap()[:, b, t0:t0 + CH], in_=vx[:]), reads=[d_vx], writes=[d_vxd])
            ms.op("act", lambda e: e.activation(out=vxb[:], in_=vx[:], func=AF.Copy), reads=[d_vx], writes=[d_vxb])
            bank, d_bank = cm.banks[4], cm.d_bank[4]
            bview = bank[:].bitcast(BF16)

            def tr(e, bview=bview):
                ins = None
                for j in range(NBK):
                    ins = e.transpose(out=bview[:, j * 128:(j + 1) * 128], in_=vxb[:, j * 128:(j + 1) * 128], identity=cm.ident_b[:])
                return ins
            ms.op("pe", tr, reads=[d_vxb, cm.d_ident], writes=[d_bank])
            ms.op("act", lambda e, bview=bview: e.activation(out=vtt[:], in_=bview[:, 0:CH], func=AF.Copy), reads=[d_bank], writes=[d_vtt])
            for m in range(CH // 512):
                bk, d_bk = cm.banks[5 + m % 2], cm.d_bank[5 + m % 2]
                ms.op("pe", lambda e, bk=bk, m=m: e.matmul(bk[:, :], lhsT=Jb[:], rhs=vtt[:, m * 512:(m + 1) * 512], start=True, stop=True),
                      reads=[d_vtt, d_w], writes=[d_bk])
                col0 = b * 64 + ck * NBK + m * 4
                ms.op("dve", lambda e, bk=bk, col0=col0: e.tensor_copy(out=V_rev[:, :, col0:col0 + 4].rearrange("p c k -> p k c"),
                                                                     in_=bk[:, :].rearrange("p (k c) -> p k c", c=128)),
                      reads=[d_bk], writes=[d_Vrev])

    ksk = [cx.sb(f"ksk{i}", [128, KW], BF16) for i in range(2)]
    d_ksk = [Dep() for _ in range(2)]
    Y_T = big[:].rearrange("p (n c) -> p n c", c=128)
    order = [0] + [d for k in range(1, 64) for d in (k, -k)]
    for c in range(n_ch):
        kt, d_kt = ksk[c % 2], d_ksk[c % 2]
        src = bass.AP(tensor=kk_d, offset=c * NLAG + 1, ap=[[1, 128], [1, KW]])
        ms.dma("sp" if c % 2 == 0 else "act", lambda e, kt=kt, src=src: e.dma_start(out=kt[:], in_=src), reads=[d_kkd], writes=[d_kt])
        slot = c % 4
        bank, d_bank = cm.banks[(c // 4) % 2], cm.d_bank[(c // 4) % 2]
        ob = bank[:, slot * 128:(slot + 1) * 128].rearrange("p (b i) -> p b i", b=2)
        vb = V_rev[:, c, :].rearrange("p (b i) -> p b i", b=2)

        def lc(e, kt=kt, ob=ob, vb=vb):
            ins = None
            for n_, d in enumerate(order):
                lo, hi = max(0, d), min(63, 63 + d) + 1
                m0 = (d + 63) * 128
                ins = e.matmul(ob[:, :, lo:hi], lhsT=kt[:, m0:m0 + 128], rhs=vb[:, :, lo - d:hi - d], start=(n_ == 0), stop=(n_ == len(order) - 1))
            return ins
        ms.op("pe", lc, reads=[d_kt, d_Vrev], writes=[d_bank])
        if slot == 3 or c == n_ch - 1:
            c0 = c - slot
            ms.op("dve", lambda e, bank=bank, c0=c0, slot=slot: e.tensor_copy(
                out=Y_T[:, :, c0:c0 + slot + 1].rearrange("p n c -> p c n"),
                in_=bank[:, 0:(slot + 1) * 128].rearrange("p (c n) -> p c n", n=128)),
                reads=[d_bank], writes=[d_big])

    for b in range(2):
        for ck in range(SEQ // CH):
            t0 = ck * CH
            conv(0, b, t0, 0)
            ms.dma("sp", lambda e, b=b, t0=t0: e.dma_start(out=vx[:], in_=vx_d.ap()[:, b, t0:t0 + CH]), reads=[d_vxd], writes=[d_vx])
            for m in range(CH // 512):
                bk, d_bk = cm.banks[5 + m % 2], cm.d_bank[5 + m % 2]

                def trb(e, bk=bk, m=m, b=b, ck=ck):
                    ins = None
                    for j in range(4):
                        bi = b * 64 + ck * NBK + m * 4 + j
                        ins = e.transpose(out=bk[:, j * 128:(j + 1) * 128], in_=Y_T[:, bi, :], identity=cm.ident_f[:])
                    return ins
                ms.op("pe", trb, reads=[d_big, cm.d_ident], writes=[d_bk])
                sl = slice(m * 512, (m + 1) * 512)
                ms.op("dve", lambda e, bk=bk, sl=sl: e.scalar_tensor_tensor(out=uc[1][:, sl], in0=vx[:, sl], scalar=hb[:, 0:1], in1=bk[:, :], op0=ALU.mult, op1=ALU.add),
                      reads=[d_vx, d_bk, d_w], writes=[d_uc[1]])
            ms.op("dve", lambda e: e.tensor_tensor(out=uc[1][:], in0=uc[1][:], in1=uc[0][:], op=ALU.mult), reads=[d_uc[0], d_uc[1]], writes=[d_uc[1]])
            ms.dma("sp", lambda e, b=b, t0=t0: e.dma_start(out=o_out[:, b, t0:t0 + CH], in_=uc[1][:]), reads=[d_uc[1]], writes=[d_oo])
    cx.finish()
    return cx


def _filter_tables():
    L = SEQ
    t = np.linspace(0.0, 1.0, L, dtype=np.float32)
    w = (np.float32(2.0 * math.pi) * np.arange(L, dtype=np.float32) / np.float32(L)).astype(np.float32)
    f = np.linspace(1e-4, 15.0, 16, dtype=np.float32)
    fw = (f[None, :] * w[:, None]).astype(np.float32)
    z = np.concatenate([t[:, None], np.cos(fw), -np.sin(fw)], axis=-1).astype(np.float32)
    idx = np.arange(NLAG)
    m = np.where(idx >= L, idx - L, L - idx)
    m[0] = 0
    zt = np.ascontiguousarray(z[m].T)
    tt = t[m].copy()
    tt[0] = 1e4
    return zt.astype(np.float32), tt.astype(np.float32)


def l2h_in_maps(inp, hyTs):
    zt, tt = _filter_tables()
    ident = np.eye(128, dtype=np.float32)
    Jm = np.ascontiguousarray(ident[::-1])
    fb = np.ascontiguousarray(np.stack([inp["flt_b1"][0], inp["flt_b2"][0], inp["flt_b3"][0], inp["flt_freq"][0]], axis=1))
    maps = []
    for j in range(NCORE):
        cs = slice(j * 128, (j + 1) * 128)
        hy3 = np.empty((128, 3, 2, SEQ), np.float32)
        for b in range(2):
            for tq in range(4):
                src = hyTs[b * 4 + tq]
                for part in range(3):
                    hy3[:, part, b, tq * TOK:(tq + 1) * TOK] = src[:, part * 8 + j, :]
        cw = np.empty((128, 3, 4), np.float32)
        for part in range(3):
            cw[:, part, 0:3] = inp["conv_w"][0][:, part * 1024 + j * 128: part * 1024 + (j + 1) * 128].T
            cw[:, part, 3] = inp["conv_b"][0][part * 1024 + j * 128: part * 1024 + (j + 1) * 128]
        w4 = np.ascontiguousarray(inp["flt_w4"][0].reshape(64, 2, 1024)[:, :, cs])
        maps.append(dict(
            ident=ident, Jm=Jm, hy3=hy3, cw=cw, hb=np.ascontiguousarray(inp["hy_bias"][0][cs].reshape(128, 1)),
            fw1=np.ascontiguousarray(inp["flt_w1"][0]), fw2=np.ascontiguousarray(inp["flt_w2"][0]), fw3=np.ascontiguousarray(inp["flt_w3"][0]),
            fw4=w4, fb=fb, dec=np.ascontiguousarray(inp["flt_decay"][0][:, cs].T), zt=zt, tt=tt))
    return maps


def l2a_in_maps(r1):
    ident = np.eye(128, dtype=np.float32)
    maps = []
    for core in range(NCORE):
        b = core // 4
        grp = [r1[b * 4 + t] for t in range(4)]
        kT = np.ascontiguousarray(np.concatenate([grp[0]["kTc"]] + [g["kT"] for g in grp], axis=2))
        v = np.ascontiguousarray(np.concatenate([grp[0]["vc"]] + [g["v"] for g in grp], axis=0))
        maps.append(dict(ident=ident, qT=np.ascontiguousarray(r1[core]["qT"]), kT=kT, v=v))
    return maps


_DBG = {}


def kernel(**inputs):
    inp = {k: np.asarray(v) for k, v in inputs.items()}
    cores = list(range(NCORE))
    cx1 = build_l1(4, True)
    r1 = run_bass_kernel_spmd(cx1.nc, l1_in_maps(inp), core_ids=cores).results
    cxh = build_l2h(128)
    rh = run_bass_kernel_spmd(cxh.nc, l2h_in_maps(inp, [r["hyT"] for r in r1]), core_ids=cores).results
    cxa = build_l2a(8, 4)
    ra = run_bass_kernel_spmd(cxa.nc, l2a_in_maps(r1), core_ids=cores).results
    hyoTs = []
    for core in range(NCORE):
        b, tq = core // 4, core % 4
        hyoTs.append(np.ascontiguousarray(np.stack([rh[j]["hyo"][:, b, tq * TOK:(tq + 1) * TOK] for j in range(NCORE)], axis=1)))
    cx3 = build_l3(4)
    r3 = run_bass_kernel_spmd(cx3.nc, l3_in_maps(inp, [r["x1"] for r in r1], [r["attnT"] for r in ra], hyoTs), core_ids=cores).results
    out = np.empty((2, SEQ, D), np.float32)
    for core in range(NCORE):
        b, tq = core // 4, core % 4
        out[b, tq * TOK:(tq + 1) * TOK] = r3[core]["out"]
    if _DBG:
        _DBG.update(r1=r1, rh=rh, ra=ra, hyoTs=hyoTs, out=out)
    return out
```
